# Optimizing a Trainium2 kernel written in Bass

```python
import math
import jax, jax.numpy as jnp
from jax import lax
import numpy as np

D_MODEL = 2048
BATCH = 2
SEQ = 8192
DEPTH = 4
DEC_BATCH = 8
DEC_SEQ = 2048
PAST_LEN = 128

N_MIXERS = 2
N_RET_LAYERS = (DEPTH + 1) // 2
N_MLA_LAYERS = DEPTH // 2

ALPHA = (2.0 * DEPTH) ** 0.25
BETA = (8.0 * DEPTH) ** -0.25

RET_HEADS = D_MODEL // 256
RET_QK_DIM = 256
RET_V_DIM = 512
RET_QK_W = RET_HEADS * RET_QK_DIM
RET_V_W = RET_HEADS * RET_V_DIM
RET_IN = 2 * RET_QK_W + 2 * RET_V_W
RET_CHUNK = 128
RET_ROPE_BASE = 10000.0

MLA_HEADS = 16
MLA_Q_RANK = 1536
MLA_KV_RANK = 512
MLA_NOPE = 128
MLA_ROPE = 64
MLA_V = 128
MLA_IN = MLA_Q_RANK + MLA_KV_RANK + MLA_ROPE
MLA_ROPE_BASE = 10000.0
Q_BLOCK = 128

FFN_DIM = 5632
CONV_WIDTH = 3

LN_EPS = 1e-5
RMS_EPS = 1e-6
GN_EPS = 1e-6

kernel_name = "hybrid_retention_mla_convffn_encoder"


def layer_norm(x, g, b):
    xf = x.astype(jnp.float32)
    mu = jnp.mean(xf, axis=-1, keepdims=True)
    var = jnp.mean(jnp.square(xf - mu), axis=-1, keepdims=True)
    y = (xf - mu) * lax.rsqrt(var + LN_EPS) * g.astype(jnp.float32) + b.astype(jnp.float32)
    return y.astype(x.dtype)


def rms_norm(x, g):
    xf = x.astype(jnp.float32)
    y = xf * lax.rsqrt(jnp.mean(jnp.square(xf), axis=-1, keepdims=True) + RMS_EPS)
    return (y * g.astype(jnp.float32)).astype(x.dtype)


def rope_tables(seq, dim, base):
    inv = 1.0 / (base ** (jnp.arange(0, dim // 2, dtype=jnp.float32) * (2.0 / dim)))
    ang = jnp.arange(seq, dtype=jnp.float32)[:, None] * inv[None, :]
    return jnp.cos(ang), jnp.sin(ang)


def apply_rope(x, cos, sin):
    half = x.shape[-1] // 2
    x1 = x[..., :half].astype(jnp.float32)
    x2 = x[..., half:].astype(jnp.float32)
    y = jnp.concatenate([x1 * cos - x2 * sin, x1 * sin + x2 * cos], axis=-1)
    return y.astype(x.dtype)


def retention_scan(q, k, v, log_g, strict):
    B, H, S, dk = q.shape
    dv = v.shape[-1]
    n_chunks = S // RET_CHUNK
    idx = jnp.arange(RET_CHUNK, dtype=jnp.float32)
    rel = idx[:, None] - idx[None, :]
    mask = (rel > 0) if strict else (rel >= 0)
    d_intra = jnp.where(mask[None], jnp.exp(log_g[:, None, None] * jnp.maximum(rel, 0.0)[None]), 0.0)
    xi = jnp.exp(log_g[:, None] * (idx[None, :] + 1.0))[:, :, None]
    zeta = jnp.exp(log_g[:, None] * (RET_CHUNK - 1.0 - idx[None, :]))[:, :, None]
    g_chunk = jnp.exp(log_g * RET_CHUNK)[:, None, None]

    def to_chunks(t):
        return t.reshape(B, H, n_chunks, RET_CHUNK, t.shape[-1]).transpose(2, 0, 1, 3, 4)

    def step(state, inp):
        qc, kc, vc = inp
        s = jnp.einsum('bhid,bhjd->bhij', qc, kc) * d_intra
        o = jnp.einsum('bhij,bhjv->bhiv', s, vc) + jnp.einsum('bhid,bhdv->bhiv', qc * xi, state)
        state = g_chunk * state + jnp.einsum('bhjd,bhjv->bhdv', kc * zeta, vc)
        return state, o

    state0 = jnp.zeros((B, H, dk, dv), jnp.float32)
    _, o = lax.scan(step, state0, (to_chunks(q), to_chunks(k), to_chunks(v)))
    return o.transpose(1, 2, 0, 3, 4).reshape(B, H, S, dv)


def retention_mixer(x, w_in, decay_fwd, decay_bwd, w_out):
    B, S, _ = x.shape
    p = x @ w_in
    q = p[..., :RET_QK_W].reshape(B, S, RET_HEADS, RET_QK_DIM)
    k = p[..., RET_QK_W:2 * RET_QK_W].reshape(B, S, RET_HEADS, RET_QK_DIM)
    v = p[..., 2 * RET_QK_W:2 * RET_QK_W + RET_V_W].reshape(B, S, RET_HEADS, RET_V_DIM)
    gate = p[..., 2 * RET_QK_W + RET_V_W:]
    cos, sin = rope_tables(S, RET_QK_DIM, RET_ROPE_BASE)
    q = apply_rope(q, cos[:, None], sin[:, None])
    k = apply_rope(k, cos[:, None], sin[:, None])
    qf = q.astype(jnp.float32).transpose(0, 2, 1, 3)
    kf = k.astype(jnp.float32).transpose(0, 2, 1, 3) * (RET_QK_DIM ** -0.5)
    vf = v.astype(jnp.float32).transpose(0, 2, 1, 3)
    log_f = jax.nn.log_sigmoid(decay_fwd.astype(jnp.float32))
    log_b = jax.nn.log_sigmoid(decay_bwd.astype(jnp.float32))
    o_fwd = retention_scan(qf, kf, vf, log_f, False)
    o_bwd = jnp.flip(retention_scan(jnp.flip(qf, 2), jnp.flip(kf, 2), jnp.flip(vf, 2), log_b, True), 2)
    o = o_fwd + o_bwd
    mu = jnp.mean(o, axis=-1, keepdims=True)
    var = jnp.mean(jnp.square(o - mu), axis=-1, keepdims=True)
    o = (o - mu) * lax.rsqrt(var + GN_EPS)
    o = o.transpose(0, 2, 1, 3).reshape(B, S, RET_V_W).astype(x.dtype)
    return (jax.nn.silu(gate) * o) @ w_out


def mla_mixer(x, w_in, q_norm, kv_norm, w_uq, w_ukv, w_out):
    B, S, _ = x.shape
    c = x @ w_in
    cq = rms_norm(c[..., :MLA_Q_RANK], q_norm)
    ckv = rms_norm(c[..., MLA_Q_RANK:MLA_Q_RANK + MLA_KV_RANK], kv_norm)
    k_rope = c[..., MLA_Q_RANK + MLA_KV_RANK:]
    q = (cq @ w_uq).reshape(B, S, MLA_HEADS, MLA_NOPE + MLA_ROPE)
    kv = (ckv @ w_ukv).reshape(B, S, MLA_HEADS, MLA_NOPE + MLA_V)
    q_nope, q_rope = q[..., :MLA_NOPE], q[..., MLA_NOPE:]
    k_nope, v = kv[..., :MLA_NOPE], kv[..., MLA_NOPE:]
    cos, sin = rope_tables(S, MLA_ROPE, MLA_ROPE_BASE)
    q_rope = apply_rope(q_rope, cos[:, None], sin[:, None])
    k_rope = apply_rope(k_rope, cos, sin)
    scale = (MLA_NOPE + MLA_ROPE) ** -0.5
    n_blocks = S // Q_BLOCK
    qn_blocks = q_nope.reshape(B, n_blocks, Q_BLOCK, MLA_HEADS, MLA_NOPE).transpose(1, 0, 2, 3, 4)
    qr_blocks = q_rope.reshape(B, n_blocks, Q_BLOCK, MLA_HEADS, MLA_ROPE).transpose(1, 0, 2, 3, 4)

    def attend(args):
        qn, qr = args
        s = jnp.einsum('bqhd,bkhd->bhqk', qn, k_nope) + jnp.einsum('bqhr,bkr->bhqk', qr, k_rope)
        pr = jax.nn.softmax(s.astype(jnp.float32) * scale, axis=-1).astype(v.dtype)
        return jnp.einsum('bhqk,bkhd->bqhd', pr, v)

    o = lax.map(attend, (qn_blocks, qr_blocks))
    o = o.transpose(1, 0, 2, 3, 4).reshape(B, S, MLA_HEADS * MLA_V)
    return o @ w_out


def conv_ffn(x, w_up, conv_w, conv_b, w_down):
    h = x @ w_up
    u, g = h[..., :FFN_DIM], h[..., FFN_DIM:]
    gp = jnp.pad(g, ((0, 0), (1, 1), (0, 0)))
    g = gp[:, :-2] * conv_w[0] + gp[:, 1:-1] * conv_w[1] + gp[:, 2:] * conv_w[2] + conv_b
    return (jax.nn.silu(g) * u) @ w_down


def trunk(x, ret_w_in, ret_decay_fwd, ret_decay_bwd, ret_w_out,
          mla_w_in, mla_q_norm, mla_kv_norm, mla_w_uq, mla_w_ukv, mla_w_out,
          ln1_g, ln1_b, ln2_g, ln2_b, ffn_w_up, ffn_conv_w, ffn_conv_b, ffn_w_down):
    for i in range(DEPTH):
        j = i // N_MIXERS
        if i % N_MIXERS == 0:
            m = retention_mixer(x, ret_w_in[j], ret_decay_fwd[j], ret_decay_bwd[j], ret_w_out[j])
        else:
            m = mla_mixer(x, mla_w_in[j], mla_q_norm[j], mla_kv_norm[j],
                          mla_w_uq[j], mla_w_ukv[j], mla_w_out[j])
        x = layer_norm(ALPHA * x + m, ln1_g[i], ln1_b[i])
        f = conv_ffn(x, ffn_w_up[i], ffn_conv_w[i], ffn_conv_b[i], ffn_w_down[i])
        x = layer_norm(ALPHA * x + f, ln2_g[i], ln2_b[i])
    return x


def _dense(k, shape, fan_in, scale=1.0):
    return jax.random.normal(k, shape, jnp.float32) * (scale * fan_in ** -0.5)


def setup_inputs(seed: int = 0) -> dict:
    key = jax.random.key(seed)
    ks = jax.random.split(key, 20)
    expo = 5.0 + jnp.arange(RET_HEADS, dtype=jnp.float32)
    decay_logit = jnp.log(jnp.exp2(expo) - 1.0)
    nrm = lambda k, shape: jax.random.normal(k, shape, jnp.float32)
    return {
        "x_prompt": nrm(ks[0], (BATCH, SEQ, D_MODEL)),
        "x_sample": nrm(ks[1], (DEC_BATCH, DEC_SEQ, D_MODEL)),
        "ret_w_in": _dense(ks[2], (N_RET_LAYERS, D_MODEL, RET_IN), D_MODEL),
        "ret_decay_fwd": decay_logit + 0.05 * nrm(ks[3], (N_RET_LAYERS, RET_HEADS)),
        "ret_decay_bwd": decay_logit + 0.05 * nrm(ks[4], (N_RET_LAYERS, RET_HEADS)),
        "ret_w_out": _dense(ks[5], (N_RET_LAYERS, RET_V_W, D_MODEL), RET_V_W, BETA),
        "mla_w_in": _dense(ks[6], (N_MLA_LAYERS, D_MODEL, MLA_IN), D_MODEL),
        "mla_q_norm": 1.0 + 0.02 * nrm(ks[7], (N_MLA_LAYERS, MLA_Q_RANK)),
        "mla_kv_norm": 1.0 + 0.02 * nrm(ks[8], (N_MLA_LAYERS, MLA_KV_RANK)),
        "mla_w_uq": _dense(ks[9], (N_MLA_LAYERS, MLA_Q_RANK, MLA_HEADS * (MLA_NOPE + MLA_ROPE)), MLA_Q_RANK),
        "mla_w_ukv": _dense(ks[10], (N_MLA_LAYERS, MLA_KV_RANK, MLA_HEADS * (MLA_NOPE + MLA_V)), MLA_KV_RANK),
        "mla_w_out": _dense(ks[11], (N_MLA_LAYERS, MLA_HEADS * MLA_V, D_MODEL), MLA_HEADS * MLA_V, BETA),
        "ln1_g": 1.0 + 0.02 * nrm(ks[12], (DEPTH, D_MODEL)),
        "ln1_b": 0.02 * nrm(ks[13], (DEPTH, D_MODEL)),
        "ln2_g": 1.0 + 0.02 * nrm(ks[14], (DEPTH, D_MODEL)),
        "ln2_b": 0.02 * nrm(ks[15], (DEPTH, D_MODEL)),
        "ffn_w_up": _dense(ks[16], (DEPTH, D_MODEL, 2 * FFN_DIM), D_MODEL),
        "ffn_conv_w": _dense(ks[17], (DEPTH, CONV_WIDTH, FFN_DIM), CONV_WIDTH),
        "ffn_conv_b": 0.02 * nrm(ks[18], (DEPTH, FFN_DIM)),
        "ffn_w_down": _dense(ks[19], (DEPTH, FFN_DIM, D_MODEL), FFN_DIM, BETA),
    }


def reference(x_prompt, x_sample, ret_w_in, ret_decay_fwd, ret_decay_bwd, ret_w_out,
              mla_w_in, mla_q_norm, mla_kv_norm, mla_w_uq, mla_w_ukv, mla_w_out,
              ln1_g, ln1_b, ln2_g, ln2_b, ffn_w_up, ffn_conv_w, ffn_conv_b, ffn_w_down):
    weights = (ret_w_in, ret_decay_fwd, ret_decay_bwd, ret_w_out,
               mla_w_in, mla_q_norm, mla_kv_norm, mla_w_uq, mla_w_ukv, mla_w_out,
               ln1_g, ln1_b, ln2_g, ln2_b, ffn_w_up, ffn_conv_w, ffn_conv_b, ffn_w_down)
    y_prompt = trunk(x_prompt, *weights)
    y_sample = trunk(x_sample, *weights)
    return (y_prompt, y_sample)
```

```python
import numpy as np
from contextlib import ExitStack
import concourse.bass as bass
import concourse.mybir as mybir
from concourse.bass_utils import run_bass_kernel_spmd

F32 = mybir.dt.float32
BF16 = mybir.dt.bfloat16
AF = mybir.ActivationFunctionType
ALU = mybir.AluOpType

D = 2048
NT = 4096
SEG = 2048
TT = 512
DEPTH = 4
ALPHA = (2.0 * DEPTH) ** 0.25
LN_EPS = 1e-5
RMS_EPS = 1e-6
GN_EPS = 1e-6
FFN = 5632
NFC = 44
MLA_SCALE = 192.0 ** -0.5

C_RELF, C_MF, C_RELB, C_MB = 0, 128, 256, 384
C_IXF, C_IXB = 512, 640
C_COLF, C_COLB = 768, 769
NR = 8
C_EF, C_MFC, C_EB, C_MBC = 770, 778, 786, 794
C_ID = 802
C_SEL = 930
NCONST = 932
XG = [list(range(8))]
KMASK = -30000.0


class Buf:
    __slots__ = ("w", "r", "excl")

    def __init__(self):
        self.w = None
        self.r = {}
        self.excl = False


class T:
    def __init__(self, t):
        self.t = t
        self.b = Buf()


class Sched:
    sim = False
    local4 = False

    def __init__(self, nc):
        self.nc = nc
        self.E = {"pe": nc.tensor, "act": nc.scalar, "dve": nc.vector, "pool": nc.gpsimd, "sp": nc.sync}
        self.sem = {e: nc.alloc_semaphore("sem_" + e) for e in ("pe", "act", "dve", "pool")}
        self.cnt = {e: 0 for e in self.sem}
        self.waited = {e: {} for e in self.E}
        self.dq = {}
        for q, n in (("sp", 24), ("pool", 8)):
            self.dq[q] = {"slots": [[nc.alloc_semaphore("dma_%s_%d" % (q, i)), 0] for i in range(n)], "i": 0}
        self.cc = []
        self.trace = {e: [] for e in self.E}

    def _wait(self, e, deps):
        need = {}
        for d in deps:
            if d is None:
                continue
            sem, val, src = d
            if src == "pe" and e == "pe":
                continue
            k = id(sem)
            if k not in need or val > need[k][1]:
                need[k] = (sem, val)
        w = self.waited[e]
        for k, (sem, val) in need.items():
            if w.get(k, 0) >= val:
                continue
            self.E[e].wait_ge(sem, val)
            self.trace[e].append(("w", k, val))
            w[k] = val

    @staticmethod
    def _deps(r, w, e=None):
        deps = []
        for b in r:
            deps.append(b.w)
            if b.excl:
                deps.extend(t for t in b.r.values() if t[2] != e)
        for b in w:
            deps.append(b.w)
            deps.extend(b.r.values())
        return deps

    @staticmethod
    def _record(tok, r, w):
        k = id(tok[0])
        for b in r:
            old = b.r.get(k)
            if old is None or old[1] < tok[1]:
                b.r[k] = tok
        for b in w:
            b.w = tok
            b.r = {}

    def op(self, e, fn, r=(), w=(), inc=True):
        self._wait(e, self._deps(r, w, e))
        ins = fn()
        if inc:
            self.cnt[e] += 1
            ins.then_inc(self.sem[e], 1)
            self.trace[e].append(("i", id(self.sem[e]), 1))
            tok = (self.sem[e], self.cnt[e], e)
        else:
            tok = (self.sem[e], self.cnt[e] + 1, e)
        self._record(tok, r, w)
        return tok

    def dma(self, q, out, in_, r=(), w=()):
        dq = self.dq[q]
        slot = dq["slots"][dq["i"] % len(dq["slots"])]
        dq["i"] += 1
        deps = self._deps(r, w)
        if slot[1] > 0:
            deps.append((slot[0], slot[1], "dma"))
        self._wait(q, deps)
        ins = self.E[q].dma_start(out=out, in_=in_)
        slot[1] += 16
        ins.then_inc(slot[0], 16)
        self.trace[q].append(("i", id(slot[0]), 16))
        tok = (slot[0], slot[1], "dma")
        self._record(tok, r, w)
        return tok

    def collective(self, kind, groups, in_ap, out_ap, r=(), w=()):
        if len(groups) > 1:
            self.nsmall = getattr(self, "nsmall", 0) + 1
        if self.sim or (self.local4 and len(groups) > 1 and self.nsmall > self.nreal):
            rows = int(in_ap.shape[0])
            step = max(1, (1 << 21) // (int(in_ap.shape[1]) * 2))
            tok = None
            for k in range(int(out_ap.shape[0]) // rows):
                for r0 in range(0, rows, step):
                    r1 = min(rows, r0 + step)
                    tok = self.dma("sp", out_ap[k * rows + r0:k * rows + r1, :], in_ap[r0:r1, :], r=r, w=w)
            return tok
        sem = self.nc.alloc_semaphore("cc_%d" % len(self.cc))
        self._wait("pool", self._deps(r, w))
        ins = self.nc.gpsimd.collective_compute(kind, ALU.bypass, replica_groups=groups,
                                                ins=[in_ap.opt()], outs=[out_ap.opt()])
        ins.then_inc(sem)
        self.trace["pool"].append(("i", id(sem), 1))
        tok = (sem, 1, "cc")
        self.cc.append(tok)
        self._record(tok, r, w)
        return tok

    def all_tokens(self):
        toks = [(self.sem[e], self.cnt[e], "x") for e in self.sem if self.cnt[e] > 0]
        for q in self.dq.values():
            for s in q["slots"]:
                if s[1] > 0:
                    toks.append((s[0], s[1], "dma"))
        return toks

    def barrier(self, engines=None):
        toks = self.all_tokens()
        for e in (engines or self.E):
            self._wait(e, toks)


class Pool:
    def __init__(self, items):
        self.items = items
        self.i = 0

    def next(self):
        it = self.items[self.i % len(self.items)]
        self.i += 1
        return it


def _kgroups(kc, maxg):
    n = (kc + maxg - 1) // maxg
    g = kc // n
    assert g * n == kc
    return [(i * g, g) for i in range(n)]


class Builder:
    def __init__(self, stop_after=None, debug=(), sim=False):
        self.sim = sim
        self.nsh = 1 if sim else 8
        self.stop_after = stop_after
        self.debug = debug
        self.nc = nc = bass.Bass("TRN2", target_bir_lowering=False)
        self.S = Sched(nc)
        self.S.sim = sim
        import os as _os
        self.S.local4 = bool(_os.environ.get("KDBG_LOCALCC"))
        self.S.nreal = int(_os.environ.get("KDBG_NREAL", "0"))
        self.order = []
        for i in range(DEPTH):
            j = i // 2
            if i % 2 == 0:
                self.order += [("ret_w_in", j), ("ret_w_out", j)]
            else:
                self.order += [("mla_w_in", j), ("mla_w_uq", j), ("mla_w_ukv", j), ("mla_w_out", j)]
            self.order += [("ffn_w_up", i), ("ffn_w_down", i)]
        if stop_after is not None:
            self.order = [o for o in self.order if self._needed(o, stop_after)]
        self._declare()
        self.deferred = []

    def _declare(self):
        nc = self.nc

        def din(name, shape, dt=F32):
            return nc.dram_tensor(name, list(shape), dt, kind="ExternalInput")

        def dsc(name, shape, dt, shared=False):
            if shared:
                return nc.dram_tensor(name, list(shape), dt, addr_space="Shared")
            return nc.dram_tensor(name, list(shape), dt)

        self.x_in = din("x", [NT, D])
        self.y_out = nc.dram_tensor("y", [NT, D], F32, kind="ExternalOutput")
        self.wspec = {
            "ret_w_in": (2, 2048, 12288), "ret_w_out": (2, 4096, 2048),
            "mla_w_in": (2, 2048, 2112), "mla_w_uq": (2, 1536, 4096), "mla_w_ukv": (2, 512, 4096),
            "mla_w_out": (2, 2048, 2048), "ffn_w_up": (4, 2048, 11264), "ffn_w_down": (4, 5632, 2048),
        }
        self.w_in = {}
        self.w_loc = {}
        self.w_g = {}
        for name, (L, K, Fd) in self.wspec.items():
            self.w_in[name] = [None] * L
            self.w_loc[name] = [None] * L
            self.w_g[name] = [None] * L
        for name, l in self.order:
            L, K, Fd = self.wspec[name]
            self.w_in[name][l] = din("%s_%d" % (name, l), [K // self.nsh, Fd])
            self.w_loc[name][l] = dsc("wl_%s_%d" % (name, l), [K // self.nsh, Fd], BF16)
            self.w_g[name][l] = T(dsc("wg_%s_%d" % (name, l), [K, Fd], BF16, shared=not self.sim))
        self.ret_decay = din("ret_decay", [2, 128, 16])
        self.q_norm = din("mla_q_norm", [2, 128, 1536])
        self.kv_norm = din("mla_kv_norm", [2, 128, 512])
        self.ln = din("ln", [4, 4, 128, 2048])
        self.conv = din("conv", [4, 128, NFC * 4])
        self.rope_ret = din("rope_ret", [2, 128, NT])
        self.rope_mla_tm = din("rope_mla_tm", [NT, 64])
        self.rope_mla_fm = din("rope_mla_fm", [2, 64, NT])
        self.consts = din("consts", [128, NCONST])
        self.X = dsc("X", [NT, D], F32)
        self.XT = dsc("XT", [D, NT], BF16)
        self.QT = dsc("QT", [2048, NT], BF16)
        self.KT = dsc("KT", [2048, NT], BF16)
        self.V = dsc("V", [NT, 4096], BF16)
        self.GATE = dsc("GATE", [NT, 4096], F32)
        self.OGT = dsc("OGT", [4096, NT], BF16)
        self.LST = [dsc("LST%d" % h, [2 * 256, 512], F32) for h in range(8)]
        self.LSTG = [dsc("LSTG%d" % h, [NR * 512, 512], F32, shared=not self.sim) for h in range(8)]
        self.CQT = dsc("CQT", [1536, NT], BF16)
        self.CKR_S = dsc("CKR_S", [576, SEG], BF16)
        self.CKR_P = [dsc("CKR_P%d" % j, [576, TT], BF16) for j in range(4)]
        self.CKR_PG = [dsc("CKR_PG%d" % j, [NR * 576, TT], BF16, shared=not self.sim) for j in range(4)]
        self.kmask = din("kmask", [NR, SEG])
        self.QT2 = dsc("QT2", [16 * 192, NT], BF16)
        self.KNT_S = dsc("KNT_S", [2048, SEG], BF16)
        self.KNT_P = dsc("KNT_P", [2048, NR * SEG], BF16)
        self.VTM_S = dsc("VTM_S", [SEG, 2048], BF16)
        self.VTM_P = dsc("VTM_P", [NR * SEG, 2048], BF16)
        self.OT = dsc("OT", [2048, NT], BF16)
        self.HHT = dsc("HHT", [FFN, NT], BF16)
        self.EXH_IN = dsc("EXH_IN", [2, D], F32)
        self.EXH = dsc("EXH", [2 * NR, D], F32, shared=not self.sim)
        self.dbg_out = {}

    def sb(self, es, name, shape, dt):
        self._uid = getattr(self, "_uid", 0) + 1
        return T(es.enter_context(self.nc.sbuf_tensor("%s_u%d" % (name, self._uid), list(shape), dt)))

    def run_deferred(self, n=1):
        for _ in range(n):
            if self.deferred:
                self.deferred.pop(0)()

    def flush_deferred(self):
        while self.deferred:
            self.deferred.pop(0)()

    def setup_globals(self, es):
        nc, S = self.nc, self.S
        self.ps = [T(nc.alloc_psum_tensor("ps%d" % i, [128, 512], F32)) for i in range(8)]
        for p in self.ps:
            p.bf = p.t.ap().bitcast(BF16)
            p.b.excl = True
        self.cst = self.sb(es, "cst", [128, NCONST], F32)
        S.dma("sp", self.cst.t[:, :], self.consts[:, :], w=[self.cst.b])
        self.ident = self.sb(es, "ident", [128, 128], BF16)
        S.op("dve", lambda: nc.vector.tensor_copy(out=self.ident.t[:, :], in_=self.cst.t[:, C_ID:C_ID + 128]),
             r=[self.cst.b], w=[self.ident.b])
        self.ones = self.sb(es, "ones", [128, 128], BF16)
        S.op("dve", lambda: nc.vector.memset(self.ones.t[:, :], 1.0), w=[self.ones.b])
        self.epsb = self.sb(es, "epsb", [128, 4], F32)
        S.op("dve", lambda: nc.vector.memset(self.epsb.t[:, 0:1], LN_EPS), w=[self.epsb.b])
        S.op("dve", lambda: nc.vector.memset(self.epsb.t[:, 1:2], RMS_EPS), w=[self.epsb.b])
        S.op("dve", lambda: nc.vector.memset(self.epsb.t[:, 2:3], GN_EPS), w=[self.epsb.b])

    def rstd(self, out_ap, var_ap, eps, r, w, scale=1.0):
        nc = self.nc
        col = {LN_EPS: 0, RMS_EPS: 1}[eps] if eps != GN_EPS or GN_EPS != RMS_EPS else 1
        eb = self.epsb.t[:, col:col + 1]
        self.S.op("act", lambda: nc.scalar.activation(out=out_ap, in_=var_ap, func=AF.Sqrt, scale=scale, bias=eb),
                  r=list(r) + [self.epsb.b], w=w)
        self.S.op("dve", lambda: nc.vector.reciprocal(out=out_ap, in_=out_ap), r=w, w=w)

    def prep_weights(self, order):
        nc, S = self.nc, self.S
        for name, l in order:
            L, K, Fd = self.wspec[name]
            c = 1056 if Fd == 2112 else 1024
            src = self.w_in[name][l][:, :].rearrange("r (n c) -> (r n) c", c=c)
            dst = self.w_loc[name][l][:, :].rearrange("r (n c) -> (r n) c", c=c)
            loc = Buf()
            S.dma("pool", dst, src, w=[loc])
            self._wl_buf = getattr(self, "_wl_buf", {})
            self._wl_buf[(name, l)] = loc
        for name, l in order:
            loc = self._wl_buf[(name, l)]
            g = self.w_g[name][l]
            S.collective("AllGather", [list(range(8))], self.w_loc[name][l].ap(), g.t.ap(),
                         r=[loc], w=[g.b])

    def emit_xT(self, src, tgi, xbf, stage, psb):
        nc, S = self.nc, self.S
        S.op("act", lambda: nc.scalar.copy(out=xbf.t[:, :], in_=src.t[:, :]), r=[src.b], w=[xbf.b])
        for q in range(4):
            pb = psb.next()
            for j in range(4):
                kc = q * 4 + j
                S.op("pe", lambda: nc.tensor.transpose(out=pb.bf[:, j * 128:(j + 1) * 128],
                                                       in_=xbf.t[:, kc * 128:(kc + 1) * 128],
                                                       identity=self.ident.t[:, :]),
                     r=[xbf.b, self.ident.b], w=[pb.b], inc=(j == 3))
            eng = "dve" if q % 2 == 0 else "act"
            o = stage.t[:, q * 4:(q + 1) * 4, tgi * 128:(tgi + 1) * 128]
            i = pb.bf[:, 0:512].rearrange("p (a b) -> p a b", a=4)
            if eng == "dve":
                S.op("dve", lambda: nc.vector.tensor_copy(out=o, in_=i), r=[pb.b], w=[stage.b])
            else:
                S.op("act", lambda: nc.scalar.copy(out=o, in_=i), r=[pb.b], w=[stage.b])

    def phase_xt0(self):
        nc, S = self.nc, self.S
        with ExitStack() as es:
            xin = Pool([self.sb(es, "x0_%d" % i, [128, D], F32) for i in range(3)])
            xbf = Pool([self.sb(es, "x0b_%d" % i, [128, D], BF16) for i in range(2)])
            stg = Pool([self.sb(es, "x0s_%d" % i, [128, 16, TT], BF16) for i in range(2)])
            psb = Pool(self.ps[0:4])
            for tt in range(NT // TT):
                st = stg.next()
                for tgi in range(4):
                    r0 = tt * TT + tgi * 128
                    xi = xin.next()
                    S.dma("sp", xi.t[:, :], self.x_in[r0:r0 + 128, :], w=[xi.b])
                    self.emit_xT(xi, tgi, xbf.next(), st, psb)
                S.dma("sp", self.XT[:, tt * TT:(tt + 1) * TT].rearrange("(kc p) t -> p kc t", p=128), st.t[:, :, :],
                      r=[st.b])
            S.barrier()

    def phase_outproj_ln(self, INT, K, wg, x_src, x_dst, ln_l, ln_j, write_xt=True):
        nc, S = self.nc, self.S
        KC = K // 128
        groups = _kgroups(KC, 22)
        gmax = max(g for _, g in groups)
        with ExitStack() as es:
            inT = Pool([self.sb(es, "op_in%d" % i, [128, gmax, TT], BF16) for i in range(2)])
            wsl = Pool([self.sb(es, "op_w%d" % i, [128, gmax, 256], BF16) for i in range(3)])
            zt = Pool([self.sb(es, "op_z%d" % i, [128, D], F32) for i in range(8)])
            gt = self.sb(es, "op_g", [128, D], F32)
            bt = self.sb(es, "op_b", [128, D], F32)
            xbf = Pool([self.sb(es, "op_xbf%d" % i, [128, D], BF16) for i in range(2)])
            stg = Pool([self.sb(es, "op_stg%d" % i, [128, 16, TT], BF16) for i in range(1)])
            sm = Pool([self.sb(es, "op_sm%d" % i, [128, 32], F32) for i in range(4)])
            psm = Pool(self.ps[0:4])
            pst = Pool(self.ps[4:8])
            S.dma("sp", gt.t[:, :], self.ln[ln_l, ln_j, :, :], w=[gt.b])
            S.dma("sp", bt.t[:, :], self.ln[ln_l, ln_j + 1, :, :], w=[bt.b])

            def ln_tail(z, tt, tgi, st, last):
                def f():
                    s = sm.next()
                    for c in range(4):
                        S.op("dve", lambda: nc.vector.bn_stats(out=s.t[:, c * 6:(c + 1) * 6],
                                                               in_=z.t[:, c * 512:(c + 1) * 512]),
                             r=[z.b], w=[s.b])
                    S.op("dve", lambda: nc.vector.bn_aggr(out=s.t[:, 24:26],
                                                          in_=s.t[:, 0:24].rearrange("p (a b) -> p a b", a=4)),
                         r=[s.b], w=[s.b])
                    self.rstd(s.t[:, 26:27], s.t[:, 25:26], LN_EPS, [s.b], [s.b])
                    S.op("dve", lambda: nc.vector.tensor_scalar(out=z.t[:, :], in0=z.t[:, :], scalar1=s.t[:, 24:25],
                                                                scalar2=s.t[:, 26:27], op0=ALU.subtract,
                                                                op1=ALU.mult), r=[z.b, s.b], w=[z.b])
                    S.op("pool", lambda: nc.gpsimd.tensor_tensor(out=z.t[:, :], in0=z.t[:, :], in1=gt.t[:, :],
                                                                 op=ALU.mult), r=[z.b, gt.b], w=[z.b])
                    S.op("dve", lambda: nc.vector.tensor_tensor(out=z.t[:, :], in0=z.t[:, :], in1=bt.t[:, :],
                                                                op=ALU.add), r=[z.b, bt.b], w=[z.b])
                    r0 = tt * TT + tgi * 128
                    S.dma("sp", x_dst[r0:r0 + 128, :], z.t[:, :], r=[z.b])
                    if write_xt:
                        self.emit_xT(z, tgi, xbf.next(), st, pst)
                        if last:
                            S.dma("sp", self.XT[:, tt * TT:(tt + 1) * TT].rearrange("(kc p) t -> p kc t", p=128),
                                  st.t[:, :, :], r=[st.b])
                return f

            for tt in range(NT // TT):
                zs = [zt.next() for _ in range(4)]
                for tgi in range(4):
                    r0 = tt * TT + tgi * 128
                    S.dma("sp", zs[tgi].t[:, :], x_src[r0:r0 + 128, :], w=[zs[tgi].b])
                for gi, (k0, kn) in enumerate(groups):
                    it = inT.next()
                    S.dma("sp", it.t[:, 0:kn, :],
                          INT[k0 * 128:(k0 + kn) * 128, tt * TT:(tt + 1) * TT].rearrange("(kc p) t -> p kc t", p=128),
                          w=[it.b])
                    for ds in range(8):
                        w = wsl.next()
                        S.dma("sp", w.t[:, 0:kn, :],
                              wg.t[k0 * 128:(k0 + kn) * 128, ds * 256:(ds + 1) * 256].rearrange(
                                  "(kc p) c -> p kc c", p=128), r=[wg.b], w=[w.b])
                        for tgi in range(4):
                            pb = psm.next()
                            for kc in range(kn):
                                S.op("pe", lambda: nc.tensor.matmul(pb.t[:, 0:256],
                                                                    lhsT=it.t[:, kc, tgi * 128:(tgi + 1) * 128],
                                                                    rhs=w.t[:, kc, :], start=(kc == 0),
                                                                    stop=(kc == kn - 1)),
                                     r=[it.b, w.b], w=[pb.b], inc=(kc == kn - 1))
                            z = zs[tgi]
                            zs_ap = z.t[:, ds * 256:(ds + 1) * 256]
                            if gi == 0:
                                S.op("dve", lambda: nc.vector.scalar_tensor_tensor(out=zs_ap, in0=zs_ap, scalar=ALPHA,
                                                                                   in1=pb.t[:, 0:256], op0=ALU.mult,
                                                                                   op1=ALU.add),
                                     r=[pb.b, z.b], w=[z.b])
                            else:
                                S.op("dve", lambda: nc.vector.tensor_tensor(out=zs_ap, in0=zs_ap, in1=pb.t[:, 0:256],
                                                                            op=ALU.add), r=[pb.b, z.b], w=[z.b])
                        self.run_deferred(1)
                st = stg.next()
                for tgi in range(4):
                    self.deferred.append(ln_tail(zs[tgi], tt, tgi, st, tgi == 3))
            self.flush_deferred()
            S.barrier()

    def phase_ret_proj(self, l):
        nc, S = self.nc, self.S
        wg = self.w_g["ret_w_in"][l]
        with ExitStack() as es:
            xT = Pool([self.sb(es, "r1_x%d" % i, [128, 16, TT], BF16) for i in range(2)])
            rp = Pool([self.sb(es, "r1_rp%d" % i, [128, 2, TT], F32) for i in range(2)])
            wsl = Pool([self.sb(es, "r1_w%d" % i, [128, 16, 512], BF16) for i in range(3)])
            tmp = Pool([self.sb(es, "r1_t%d" % i, [128, TT], F32) for i in range(4)])
            sqk = Pool([self.sb(es, "r1_sqk%d" % i, [128, 2, TT], BF16) for i in range(3)])
            sv = Pool([self.sb(es, "r1_sv%d" % i, [128, 512], BF16) for i in range(3)])
            sg = Pool([self.sb(es, "r1_sg%d" % i, [128, 512], F32) for i in range(3)])
            psqk = Pool(self.ps[0:4])
            psv = Pool(self.ps[4:8])
            for tt in range(NT // TT):
                t0 = tt * TT
                x = xT.next()
                S.dma("sp", x.t[:, :, :], self.XT[:, t0:t0 + TT].rearrange("(kc p) t -> p kc t", p=128), w=[x.b])
                rt = rp.next()
                S.dma("sp", rt.t[:, :, :], self.rope_ret[:, :, t0:t0 + TT].rearrange("a p t -> p a t"), w=[rt.b])
                cos, sin = rt.t[:, 0, :], rt.t[:, 1, :]
                for h in range(8):
                    for which in range(2):
                        base = which * 2048 + h * 256
                        sc = 1.0 if which == 0 else 1.0 / 16.0
                        w = wsl.next()
                        S.dma("sp", w.t[:, :, 0:256], wg.t[:, base:base + 256].rearrange("(kc p) c -> p kc c", p=128),
                              r=[wg.b], w=[w.b])
                        pa, pb = psqk.next(), psqk.next()
                        for c, pp in ((0, pa), (1, pb)):
                            for kc in range(16):
                                S.op("pe", lambda: nc.tensor.matmul(pp.t[:, :], lhsT=w.t[:, kc, c * 128:(c + 1) * 128],
                                                                    rhs=x.t[:, kc, :], start=(kc == 0), stop=(kc == 15)),
                                     r=[w.b, x.b], w=[pp.b], inc=(kc == 15))
                        st = sqk.next()
                        t1, t2 = tmp.next(), tmp.next()
                        S.op("dve", lambda: nc.vector.scalar_tensor_tensor(out=t1.t[:, :], in0=pa.t[:, :], scalar=sc,
                                                                           in1=cos, op0=ALU.mult, op1=ALU.mult),
                             r=[pa.b, rt.b], w=[t1.b])
                        S.op("dve", lambda: nc.vector.scalar_tensor_tensor(out=t2.t[:, :], in0=pb.t[:, :], scalar=sc,
                                                                           in1=sin, op0=ALU.mult, op1=ALU.mult),
                             r=[pb.b, rt.b], w=[t2.b])
                        S.op("pool", lambda: nc.gpsimd.tensor_tensor(out=st.t[:, 0, :], in0=t1.t[:, :], in1=t2.t[:, :],
                                                                     op=ALU.subtract), r=[t1.b, t2.b], w=[st.b])
                        t3, t4 = tmp.next(), tmp.next()
                        S.op("dve", lambda: nc.vector.scalar_tensor_tensor(out=t3.t[:, :], in0=pa.t[:, :], scalar=sc,
                                                                           in1=sin, op0=ALU.mult, op1=ALU.mult),
                             r=[pa.b, rt.b], w=[t3.b])
                        S.op("dve", lambda: nc.vector.scalar_tensor_tensor(out=t4.t[:, :], in0=pb.t[:, :], scalar=sc,
                                                                           in1=cos, op0=ALU.mult, op1=ALU.mult),
                             r=[pb.b, rt.b], w=[t4.b])
                        S.op("pool", lambda: nc.gpsimd.tensor_tensor(out=st.t[:, 1, :], in0=t3.t[:, :], in1=t4.t[:, :],
                                                                     op=ALU.add), r=[t3.b, t4.b], w=[st.b])
                        dst = (self.QT if which == 0 else self.KT)
                        S.dma("sp", dst[h * 256:(h + 1) * 256, t0:t0 + TT].rearrange("(c p) t -> p c t", p=128),
                              st.t[:, :, :], r=[st.b])
                for part in range(2):
                    for cs in range(8):
                        base = 4096 + part * 4096 + cs * 512
                        w = wsl.next()
                        S.dma("sp", w.t[:, :, :], wg.t[:, base:base + 512].rearrange("(kc p) c -> p kc c", p=128),
                              r=[wg.b], w=[w.b])
                        for tgi in range(4):
                            pp = psv.next()
                            for kc in range(16):
                                S.op("pe", lambda: nc.tensor.matmul(pp.t[:, :], lhsT=x.t[:, kc, tgi * 128:(tgi + 1) * 128],
                                                                    rhs=w.t[:, kc, :], start=(kc == 0), stop=(kc == 15)),
                                     r=[w.b, x.b], w=[pp.b], inc=(kc == 15))
                            r0 = t0 + tgi * 128
                            if part == 0:
                                o = sv.next()
                                S.op("act", lambda: nc.scalar.copy(out=o.t[:, :], in_=pp.t[:, :]), r=[pp.b], w=[o.b])
                                S.dma("sp", self.V[r0:r0 + 128, cs * 512:(cs + 1) * 512], o.t[:, :], r=[o.b])
                            else:
                                o = sg.next()
                                S.op("act", lambda: nc.scalar.activation(out=o.t[:, :], in_=pp.t[:, :], func=AF.Silu),
                                     r=[pp.b], w=[o.b])
                                S.dma("sp", self.GATE[r0:r0 + 128, cs * 512:(cs + 1) * 512], o.t[:, :], r=[o.b])
            S.barrier()

    def phase_ret_core(self, l):
        nc, S = self.nc, self.S
        cst = self.cst
        with ExitStack() as es:
            sb = lambda n, s, d: self.sb(es, n, s, d)
            dec = sb("r2_dec", [128, 16], F32)
            e1 = sb("r2_e1", [128, 16], F32)
            tt_ = sb("r2_tt", [128, 16], F32)
            LG = sb("r2_LG", [128, 16], F32)
            G128 = sb("r2_G128", [128, 16], F32)
            Z = sb("r2_Z", [128, 16], F32)
            DT = sb("r2_DT", [128, 8, 128], F32)
            tA = sb("r2_tA", [128, 128], F32)
            tB = sb("r2_tB", [128, 128], F32)
            CF = sb("r2_CF", [128, 16, NR], F32)
            XF = sb("r2_XF", [128, 128], F32)
            XB = sb("r2_XB", [128, 128], F32)
            qT = Pool([sb("r2_q%d" % i, [128, 2, SEG], BF16) for i in range(2)])
            kT = Pool([sb("r2_k%d" % i, [128, 2, SEG], BF16) for i in range(2)])
            vv = Pool([sb("r2_v%d" % i, [128, 16, 512], BF16) for i in range(2)])
            qf = sb("r2_qf", [128, 2, SEG], BF16)
            qb = sb("r2_qb", [128, 2, SEG], BF16)
            kf = sb("r2_kf", [128, 16, 256], BF16)
            kb = sb("r2_kb", [128, 16, 256], BF16)
            oacc = [sb("r2_oacc%d" % i, [128, 512], F32) for i in range(16)]
            ogT = sb("r2_ogT", [128, 4, SEG], BF16)
            Sf = sb("r2_Sf", [128, 2, 512], F32)
            Sb = sb("r2_Sb", [128, 2, 512], F32)
            Sfb = sb("r2_Sfb", [128, 2, 512], BF16)
            Sbb = sb("r2_Sbb", [128, 2, 512], BF16)
            Lr = Pool([sb("r2_Lr%d" % i, [128, 2, 512], F32) for i in range(2)])
            PT = Pool([sb("r2_PT%d" % i, [128, 128], BF16) for i in range(2)])
            gate = Pool([sb("r2_gt%d" % i, [128, 512], F32) for i in range(3)])
            on = Pool([sb("r2_on%d" % i, [128, 512], F32) for i in range(2)])
            og = Pool([sb("r2_og%d" % i, [128, 512], BF16) for i in range(2)])
            sm = Pool([sb("r2_sm%d" % i, [128, 16], F32) for i in range(3)])
            ps = self.ps
            pS = Pool([ps[0], ps[7]])
            pO, pU0, pU1, pKT, pOT, pO2 = ps[1], ps[2], ps[3], ps[4], ps[5], ps[6]

            S.dma("sp", dec.t[:, :], self.ret_decay[l, :, :], w=[dec.b])
            S.op("act", lambda: nc.scalar.activation(out=e1.t[:, :], in_=dec.t[:, :], func=AF.Exp, scale=-1.0),
                 r=[dec.b], w=[e1.b])
            S.op("dve", lambda: nc.vector.tensor_scalar(out=tt_.t[:, :], in0=e1.t[:, :], scalar1=0.2, scalar2=-0.25,
                                                        op0=ALU.mult, op1=ALU.add), r=[e1.b], w=[tt_.b])
            for cc_ in (1.0 / 3.0, -0.5, 1.0):
                S.op("dve", lambda: nc.vector.tensor_tensor(out=tt_.t[:, :], in0=tt_.t[:, :], in1=e1.t[:, :],
                                                            op=ALU.mult), r=[tt_.b, e1.b], w=[tt_.b])
                S.op("dve", lambda: nc.vector.tensor_scalar(out=tt_.t[:, :], in0=tt_.t[:, :], scalar1=cc_, scalar2=0.0,
                                                            op0=ALU.add, op1=ALU.add), r=[tt_.b], w=[tt_.b])
            S.op("dve", lambda: nc.vector.tensor_tensor(out=tt_.t[:, :], in0=tt_.t[:, :], in1=e1.t[:, :], op=ALU.mult),
                 r=[tt_.b, e1.b], w=[tt_.b])
            S.op("dve", lambda: nc.vector.tensor_scalar(out=LG.t[:, :], in0=tt_.t[:, :], scalar1=-1.0, scalar2=0.0,
                                                        op0=ALU.mult, op1=ALU.add), r=[tt_.b], w=[LG.b])
            S.op("act", lambda: nc.scalar.activation(out=G128.t[:, :], in_=LG.t[:, :], func=AF.Exp, scale=128.0),
                 r=[LG.b], w=[G128.b])
            for h in range(8):
                S.op("act", lambda: nc.scalar.activation(out=Z.t[:, h:h + 1], in_=cst.t[:, C_COLF:C_COLF + 1],
                                                         func=AF.Exp, scale=LG.t[:, h:h + 1]),
                     r=[LG.b, cst.b], w=[Z.b])
                S.op("act", lambda: nc.scalar.activation(out=Z.t[:, 8 + h:9 + h], in_=cst.t[:, C_COLB:C_COLB + 1],
                                                         func=AF.Exp, scale=LG.t[:, 8 + h:9 + h]),
                     r=[LG.b, cst.b], w=[Z.b])
                S.op("act", lambda: nc.scalar.activation(out=tA.t[:, :], in_=cst.t[:, C_RELF:C_RELF + 128],
                                                         func=AF.Exp, scale=LG.t[:, h:h + 1]),
                     r=[LG.b, cst.b], w=[tA.b])
                S.op("dve", lambda: nc.vector.tensor_tensor(out=tA.t[:, :], in0=tA.t[:, :],
                                                            in1=cst.t[:, C_MF:C_MF + 128], op=ALU.mult),
                     r=[tA.b, cst.b], w=[tA.b])
                S.op("act", lambda: nc.scalar.activation(out=tB.t[:, :], in_=cst.t[:, C_RELB:C_RELB + 128],
                                                         func=AF.Exp, scale=LG.t[:, 8 + h:9 + h]),
                     r=[LG.b, cst.b], w=[tB.b])
                S.op("dve", lambda: nc.vector.tensor_tensor(out=tB.t[:, :], in0=tB.t[:, :],
                                                            in1=cst.t[:, C_MB:C_MB + 128], op=ALU.mult),
                     r=[tB.b, cst.b], w=[tB.b])
                S.op("dve", lambda: nc.vector.tensor_tensor(out=DT.t[:, h, :], in0=tA.t[:, :], in1=tB.t[:, :],
                                                            op=ALU.add), r=[tA.b, tB.b], w=[DT.b])
                for dr, (ce, cm) in enumerate(((C_EF, C_MFC), (C_EB, C_MBC))):
                    S.op("act", lambda: nc.scalar.activation(out=CF.t[:, dr * 8 + h, :], in_=cst.t[:, ce:ce + NR],
                                                             func=AF.Exp, scale=LG.t[:, dr * 8 + h:dr * 8 + h + 1]),
                         r=[LG.b, cst.b], w=[CF.b])
                    S.op("dve", lambda: nc.vector.tensor_tensor(out=CF.t[:, dr * 8 + h, :], in0=CF.t[:, dr * 8 + h, :],
                                                                in1=cst.t[:, cm:cm + NR], op=ALU.mult),
                         r=[CF.b, cst.b], w=[CF.b])

            def load_kv(seg, h, need_q):
                c0 = seg * SEG
                k = kT.next()
                S.dma("sp", k.t[:, :, :], self.KT[h * 256:(h + 1) * 256, c0:c0 + SEG].rearrange("(c p) t -> p c t", p=128),
                      w=[k.b])
                v = vv.next()
                S.dma("sp", v.t[:, :, :], self.V[c0:c0 + SEG, h * 512:(h + 1) * 512].rearrange("(c p) v -> p c v", p=128),
                      w=[v.b])
                q = None
                if need_q:
                    q = qT.next()
                    S.dma("sp", q.t[:, :, :],
                          self.QT[h * 256:(h + 1) * 256, c0:c0 + SEG].rearrange("(c p) t -> p c t", p=128), w=[q.b])
                return q, k, v

            def k_transposes(k, h):
                for c in range(16):
                    for dc in range(2):
                        S.op("pe", lambda: nc.tensor.transpose(out=pKT.bf[:, dc * 128:(dc + 1) * 128],
                                                               in_=k.t[:, dc, c * 128:(c + 1) * 128],
                                                               identity=self.ident.t[:, :]),
                             r=[k.b, self.ident.b], w=[pKT.b], inc=(dc == 1))
                    S.op("act", lambda: nc.scalar.activation(out=kf.t[:, c, :], in_=pKT.bf[:, 0:256], func=AF.Identity,
                                                             scale=Z.t[:, h:h + 1]), r=[pKT.b, Z.b], w=[kf.b])
                    S.op("dve", lambda: nc.vector.tensor_scalar(out=kb.t[:, c, :], in0=pKT.bf[:, 0:256],
                                                                scalar1=Z.t[:, 8 + h:9 + h], scalar2=0.0, op0=ALU.mult,
                                                                op1=ALU.add), r=[pKT.b, Z.b], w=[kb.b])

            def state_update(St, Stb, kx, v, c, gcol, want_bf):
                for dc, pu in ((0, pU0), (1, pU1)):
                    S.op("pe", lambda: nc.tensor.matmul(pu.t[:, :], lhsT=kx.t[:, c, dc * 128:(dc + 1) * 128],
                                                        rhs=v.t[:, c, :], start=True, stop=True),
                         r=[kx.b, v.b], w=[pu.b])
                    S.op("dve", lambda: nc.vector.scalar_tensor_tensor(out=St.t[:, dc, :], in0=St.t[:, dc, :],
                                                                       scalar=G128.t[:, gcol:gcol + 1], in1=pu.t[:, :],
                                                                       op0=ALU.mult, op1=ALU.add),
                         r=[St.b, pu.b, G128.b], w=[St.b])
                if want_bf:
                    S.op("act", lambda: nc.scalar.copy(out=Stb.t[:, :, :], in_=St.t[:, :, :]), r=[St.b], w=[Stb.b])

            def zero_state(St, Stb):
                S.op("dve", lambda: nc.vector.memset(St.t[:, :, :], 0.0), w=[St.b])
                S.op("dve", lambda: nc.vector.memset(Stb.t[:, :, :], 0.0), w=[Stb.b])

            for h in range(8):
                _, k, v = load_kv(1, h, False)
                k_transposes(k, h)
                zero_state(Sf, Sfb)
                zero_state(Sb, Sbb)
                for c in range(16):
                    state_update(Sf, Sfb, kf, v, c, h, False)
                for c in range(15, -1, -1):
                    state_update(Sb, Sbb, kb, v, c, 8 + h, False)
                for dr, St in ((0, Sf), (1, Sb)):
                    r0 = dr * 256
                    S.dma("sp", self.LST[h][r0:r0 + 256, :].rearrange("(c p) v -> p c v", p=128), St.t[:, :, :],
                          r=[St.b])
            S.barrier()
            lstg = [Buf() for _ in range(8)]
            for h in range(8):
                S.collective("AllGather", XG, self.LST[h].ap(), self.LSTG[h].ap(), w=[lstg[h]])

            for seg in range(2):
                for h in range(8):
                    q, k, v = load_kv(seg, h, True)
                    S.op("act", lambda: nc.scalar.activation(out=XF.t[:, :], in_=cst.t[:, C_IXF:C_IXF + 128], func=AF.Exp,
                                                             scale=LG.t[:, h:h + 1]), r=[LG.b, cst.b], w=[XF.b])
                    S.op("act", lambda: nc.scalar.activation(out=XB.t[:, :], in_=cst.t[:, C_IXB:C_IXB + 128], func=AF.Exp,
                                                             scale=LG.t[:, 8 + h:9 + h]), r=[LG.b, cst.b], w=[XB.b])
                    for dc in range(2):
                        qv = q.t[:, dc, :].rearrange("p (c i) -> p c i", c=16)
                        S.op("dve", lambda: nc.vector.tensor_tensor(
                            out=qf.t[:, dc, :].rearrange("p (c i) -> p c i", c=16), in0=qv,
                            in1=XF.t[:, :].unsqueeze(1).broadcast_to([128, 16, 128]), op=ALU.mult),
                             r=[q.b, XF.b], w=[qf.b])
                        S.op("dve", lambda: nc.vector.tensor_tensor(
                            out=qb.t[:, dc, :].rearrange("p (c i) -> p c i", c=16), in0=qv,
                            in1=XB.t[:, :].unsqueeze(1).broadcast_to([128, 16, 128]), op=ALU.mult),
                             r=[q.b, XB.b], w=[qb.b])
                    k_transposes(k, h)
                    if seg == 0:
                        zero_state(Sf, Sfb)
                        zero_state(Sb, Sbb)
                    else:
                        for dr, St, Stb in ((0, Sf, Sfb), (1, Sb, Sbb)):
                            for r in range(NR):
                                lr = Lr.next()
                                r0 = r * 512 + dr * 256
                                S.dma("sp", lr.t[:, :, :],
                                      self.LSTG[h][r0:r0 + 256, :].rearrange("(c p) v -> p c v", p=128), r=[lstg[h]],
                                      w=[lr.b])
                                cf = CF.t[:, dr * 8 + h, r:r + 1]
                                if r == 0:
                                    S.op("dve", lambda: nc.vector.tensor_scalar(out=St.t[:, :, :], in0=lr.t[:, :, :],
                                                                                scalar1=cf, scalar2=0.0, op0=ALU.mult,
                                                                                op1=ALU.add),
                                         r=[lr.b, CF.b], w=[St.b])
                                else:
                                    S.op("dve", lambda: nc.vector.scalar_tensor_tensor(out=St.t[:, :, :], in0=lr.t[:, :, :],
                                                                                       scalar=cf, in1=St.t[:, :, :],
                                                                                       op0=ALU.mult, op1=ALU.add),
                                         r=[lr.b, CF.b, St.b], w=[St.b])
                            S.op("act", lambda: nc.scalar.copy(out=Stb.t[:, :, :], in_=St.t[:, :, :]), r=[St.b], w=[Stb.b])
                    for c in range(16):
                        sl = slice(c * 128, (c + 1) * 128)
                        pst = pS.next()
                        for dc in range(2):
                            S.op("pe", lambda: nc.tensor.matmul(pst.t[:, 0:128], lhsT=k.t[:, dc, sl], rhs=q.t[:, dc, sl],
                                                                start=(dc == 0), stop=(dc == 1)),
                                 r=[k.b, q.b], w=[pst.b], inc=(dc == 1))
                        pt = PT.next()
                        S.op("dve", lambda: nc.vector.tensor_tensor(out=pt.t[:, :], in0=pst.t[:, 0:128], in1=DT.t[:, h, :],
                                                                    op=ALU.mult), r=[pst.b, DT.b], w=[pt.b])
                        S.op("pe", lambda: nc.tensor.matmul(pO.t[:, :], lhsT=pt.t[:, :], rhs=v.t[:, c, :], start=True,
                                                            stop=False), r=[pt.b, v.b], w=[pO.b], inc=False)
                        for dc in range(2):
                            S.op("pe", lambda: nc.tensor.matmul(pO.t[:, :], lhsT=qf.t[:, dc, sl], rhs=Sfb.t[:, dc, :],
                                                                start=False, stop=(dc == 1)),
                                 r=[qf.b, Sfb.b], w=[pO.b], inc=(dc == 1))
                        S.op("act", lambda: nc.scalar.copy(out=oacc[c].t[:, :], in_=pO.t[:, :]), r=[pO.b], w=[oacc[c].b])
                        if c < 15:
                            state_update(Sf, Sfb, kf, v, c, h, True)
                    for c in range(15, -1, -1):
                        sl = slice(c * 128, (c + 1) * 128)
                        for dc in range(2):
                            S.op("pe", lambda: nc.tensor.matmul(pO2.t[:, :], lhsT=qb.t[:, dc, sl], rhs=Sbb.t[:, dc, :],
                                                                start=(dc == 0), stop=(dc == 1)),
                                 r=[qb.b, Sbb.b], w=[pO2.b], inc=(dc == 1))
                        oc = oacc[c]
                        S.op("dve", lambda: nc.vector.tensor_tensor(out=oc.t[:, :], in0=oc.t[:, :], in1=pO2.t[:, :],
                                                                    op=ALU.add), r=[oc.b, pO2.b], w=[oc.b])
                        if c > 0:
                            state_update(Sb, Sbb, kb, v, c, 8 + h, True)
                        s = sm.next()
                        S.op("dve", lambda: nc.vector.bn_stats(out=s.t[:, 0:6], in_=oc.t[:, :]), r=[oc.b], w=[s.b])
                        S.op("dve", lambda: nc.vector.bn_aggr(out=s.t[:, 6:8], in_=s.t[:, 0:6].rearrange("p (a b) -> p a b", a=1)), r=[s.b], w=[s.b])
                        self.rstd(s.t[:, 8:9], s.t[:, 7:8], GN_EPS, [s.b], [s.b])
                        o_n = on.next()
                        S.op("dve", lambda: nc.vector.tensor_scalar(out=o_n.t[:, :], in0=oc.t[:, :], scalar1=s.t[:, 6:7],
                                                                    scalar2=s.t[:, 8:9], op0=ALU.subtract, op1=ALU.mult),
                             r=[oc.b, s.b], w=[o_n.b])
                        g = gate.next()
                        r0 = seg * SEG + c * 128
                        S.dma("sp", g.t[:, :], self.GATE[r0:r0 + 128, h * 512:(h + 1) * 512], w=[g.b])
                        o_g = og.next()
                        S.op("pool", lambda: nc.gpsimd.tensor_tensor(out=o_g.t[:, :], in0=o_n.t[:, :], in1=g.t[:, :],
                                                                     op=ALU.mult), r=[o_n.b, g.b], w=[o_g.b])
                        for vc in range(4):
                            S.op("pe", lambda: nc.tensor.transpose(out=pOT.bf[:, vc * 128:(vc + 1) * 128],
                                                                   in_=o_g.t[:, vc * 128:(vc + 1) * 128],
                                                                   identity=self.ident.t[:, :]),
                                 r=[o_g.b, self.ident.b], w=[pOT.b], inc=(vc == 3))
                        S.op("act", lambda: nc.scalar.copy(out=ogT.t[:, :, sl],
                                                           in_=pOT.bf[:, 0:512].rearrange("p (a b) -> p a b", a=4)),
                             r=[pOT.b], w=[ogT.b])
                    c0 = seg * SEG
                    S.dma("sp", self.OGT[h * 512:(h + 1) * 512, c0:c0 + SEG].rearrange("(c p) t -> p c t", p=128),
                          ogT.t[:, :, :], r=[ogT.b])
            S.barrier()

    def phase_mla_c(self, l):
        nc, S = self.nc, self.S
        wg = self.w_g["mla_w_in"][l]
        with ExitStack() as es:
            sb = lambda n, s, d: self.sb(es, n, s, d)
            W = sb("m1_w", [128, 16, 2112], BF16)
            xT = Pool([sb("m1_x%d" % i, [128, 16, TT], BF16) for i in range(2)])
            qn = sb("m1_qn", [128, 1536], F32)
            kvn = sb("m1_kvn", [128, 512], F32)
            rtp = Pool([sb("m1_rt%d" % i, [128, 64], F32) for i in range(2)])
            junk = Pool([sb("m1_junk%d" % i, [128, 512], F32) for i in range(2)])
            cqn = Pool([sb("m1_cqn%d" % i, [128, 2048], BF16) for i in range(2)])
            kr = Pool([sb("m1_kr%d" % i, [128, 64], BF16) for i in range(2)])
            tsm = Pool([sb("m1_ts%d" % i, [128, 32], F32) for i in range(4)])
            sm = Pool([sb("m1_sm%d" % i, [128, 8], F32) for i in range(3)])
            stq = Pool([sb("m1_stq%d" % i, [128, 16, TT], BF16) for i in range(2)])
            stk = Pool([sb("m1_stk%d" % i, [64, TT], BF16) for i in range(2)])
            pst = Pool(self.ps[5:8])
            for c4 in range(4):
                S.dma("sp", W.t[:, :, c4 * 512:(c4 + 1) * 512],
                      wg.t[:, c4 * 512:(c4 + 1) * 512].rearrange("(kc p) c -> p kc c", p=128), r=[wg.b], w=[W.b])
            S.dma("sp", W.t[:, :, 2048:2112], wg.t[:, 2048:2112].rearrange("(kc p) c -> p kc c", p=128), r=[wg.b],
                  w=[W.b])
            S.dma("sp", qn.t[:, :], self.q_norm[l, :, :], w=[qn.b])
            S.dma("sp", kvn.t[:, :], self.kv_norm[l, :, :], w=[kvn.b])
            for tt in range(NT // TT):
                t0 = tt * TT
                x = xT.next()
                S.dma("sp", x.t[:, :, :], self.XT[:, t0:t0 + TT].rearrange("(kc p) t -> p kc t", p=128), w=[x.b])
                sq, sk = stq.next(), stk.next()
                for tgi in range(4):
                    r0 = t0 + tgi * 128
                    rt = rtp.next()
                    S.dma("sp", rt.t[:, :], self.rope_mla_tm[r0:r0 + 128, :], w=[rt.b])
                    pss = self.ps[0:5]
                    for j in range(5):
                        ncol = 512 if j < 4 else 64
                        for kc in range(16):
                            S.op("pe", lambda: nc.tensor.matmul(pss[j].t[:, 0:ncol], lhsT=x.t[:, kc, tgi * 128:(tgi + 1) * 128],
                                                                rhs=W.t[:, kc, j * 512:j * 512 + ncol], start=(kc == 0),
                                                                stop=(kc == 15)),
                                 r=[x.b, W.b], w=[pss[j].b], inc=(kc == 15))
                    s = sm.next()
                    for j in range(4):
                        jk = junk.next()
                        S.op("act", lambda: nc.scalar.activation(out=jk.t[:, :], in_=pss[j].t[:, :], func=AF.Square,
                                                                 accum_out=s.t[:, j:j + 1]), r=[pss[j].b], w=[jk.b, s.b])
                    S.op("dve", lambda: nc.vector.tensor_tensor(out=s.t[:, 4:5], in0=s.t[:, 0:1], in1=s.t[:, 1:2],
                                                                op=ALU.add), r=[s.b], w=[s.b])
                    S.op("dve", lambda: nc.vector.tensor_tensor(out=s.t[:, 4:5], in0=s.t[:, 4:5], in1=s.t[:, 2:3],
                                                                op=ALU.add), r=[s.b], w=[s.b])
                    self.rstd(s.t[:, 5:6], s.t[:, 4:5], RMS_EPS, [s.b], [s.b], scale=1.0 / 1536.0)
                    self.rstd(s.t[:, 6:7], s.t[:, 3:4], RMS_EPS, [s.b], [s.b], scale=1.0 / 512.0)
                    cn = cqn.next()
                    for j in range(4):
                        nrm = qn.t[:, j * 512:(j + 1) * 512] if j < 3 else kvn.t[:, :]
                        nb = qn.b if j < 3 else kvn.b
                        rs = s.t[:, 5:6] if j < 3 else s.t[:, 6:7]
                        S.op("dve", lambda: nc.vector.scalar_tensor_tensor(out=cn.t[:, j * 512:(j + 1) * 512],
                                                                           in0=pss[j].t[:, :], scalar=rs, in1=nrm,
                                                                           op0=ALU.mult, op1=ALU.mult),
                             r=[pss[j].b, s.b, nb], w=[cn.b])
                    k_ = kr.next()
                    p4 = pss[4]
                    ta, tb = tsm.next(), tsm.next()
                    S.op("dve", lambda: nc.vector.tensor_tensor(out=ta.t[:, :], in0=p4.t[:, 0:32], in1=rt.t[:, 0:32],
                                                                op=ALU.mult), r=[p4.b, rt.b], w=[ta.b])
                    S.op("dve", lambda: nc.vector.tensor_tensor(out=tb.t[:, :], in0=p4.t[:, 32:64], in1=rt.t[:, 32:64],
                                                                op=ALU.mult), r=[p4.b, rt.b], w=[tb.b])
                    S.op("dve", lambda: nc.vector.tensor_tensor(out=k_.t[:, 0:32], in0=ta.t[:, :], in1=tb.t[:, :],
                                                                op=ALU.subtract), r=[ta.b, tb.b], w=[k_.b])
                    tc_, td = tsm.next(), tsm.next()
                    S.op("dve", lambda: nc.vector.tensor_tensor(out=tc_.t[:, :], in0=p4.t[:, 0:32], in1=rt.t[:, 32:64],
                                                                op=ALU.mult), r=[p4.b, rt.b], w=[tc_.b])
                    S.op("dve", lambda: nc.vector.tensor_tensor(out=td.t[:, :], in0=p4.t[:, 32:64], in1=rt.t[:, 0:32],
                                                                op=ALU.mult), r=[p4.b, rt.b], w=[td.b])
                    S.op("dve", lambda: nc.vector.tensor_tensor(out=k_.t[:, 32:64], in0=tc_.t[:, :], in1=td.t[:, :],
                                                                op=ALU.add), r=[tc_.b, td.b], w=[k_.b])
                    for q4 in range(4):
                        pb = pst.next()
                        for j in range(4):
                            kc = q4 * 4 + j
                            S.op("pe", lambda: nc.tensor.transpose(out=pb.bf[:, j * 128:(j + 1) * 128],
                                                                   in_=cn.t[:, kc * 128:(kc + 1) * 128],
                                                                   identity=self.ident.t[:, :]),
                                 r=[cn.b, self.ident.b], w=[pb.b], inc=(j == 3))
                        o = sq.t[:, q4 * 4:(q4 + 1) * 4, tgi * 128:(tgi + 1) * 128]
                        i = pb.bf[:, 0:512].rearrange("p (a b) -> p a b", a=4)
                        if q4 % 2 == 0:
                            S.op("act", lambda: nc.scalar.copy(out=o, in_=i), r=[pb.b], w=[sq.b])
                        else:
                            S.op("dve", lambda: nc.vector.tensor_copy(out=o, in_=i), r=[pb.b], w=[sq.b])
                    pb = pst.next()
                    S.op("pe", lambda: nc.tensor.transpose(out=pb.bf[0:64, 0:128], in_=k_.t[:, 0:64],
                                                           identity=self.ident.t[:, :]),
                         r=[k_.b, self.ident.b], w=[pb.b])
                    S.op("act", lambda: nc.scalar.copy(out=sk.t[0:64, tgi * 128:(tgi + 1) * 128], in_=pb.bf[0:64, 0:128]),
                         r=[pb.b], w=[sk.b])
                S.dma("sp", self.CQT[:, t0:t0 + TT].rearrange("(kc p) t -> p kc t", p=128), sq.t[:, 0:12, :], r=[sq.b])
                if tt < 4:
                    ck, tl = self.CKR_S, tt * TT
                else:
                    ck, tl = self.CKR_P[tt % 4], 0
                S.dma("sp", ck[0:512, tl:tl + TT].rearrange("(kc p) t -> p kc t", p=128), sq.t[:, 12:16, :], r=[sq.b])
                S.dma("sp", ck[512:576, tl:tl + TT], sk.t[0:64, :], r=[sk.b])
            S.barrier()
            self.ckr_pg = [Buf() for _ in range(4)]
            for j in range(4):
                S.collective("AllGather", XG, self.CKR_P[j].ap(), self.CKR_PG[j].ap(), w=[self.ckr_pg[j]])

    def phase_mla_qkv(self, l):
        nc, S = self.nc, self.S
        wq = self.w_g["mla_w_uq"][l]
        wkv = self.w_g["mla_w_ukv"][l]
        with ExitStack() as es:
            sb = lambda n, s, d: self.sb(es, n, s, d)
            cq = Pool([sb("m2_cq%d" % i, [128, 12, TT], BF16) for i in range(2)])
            rp = Pool([sb("m2_rp%d" % i, [64, 2, TT], F32) for i in range(2)])
            wsl = Pool([sb("m2_w%d" % i, [128, 12, 256], BF16) for i in range(3)])
            sn = Pool([sb("m2_sn%d" % i, [128, TT], BF16) for i in range(3)])
            sr = Pool([sb("m2_sr%d" % i, [64, TT], BF16) for i in range(3)])
            tq = Pool([sb("m2_tq%d" % i, [64, TT], F32) for i in range(4)])
            Wkv = sb("m2_wkv", [128, 4, 4096], BF16)
            ck = Pool([sb("m2_ck%d" % i, [128, 4, TT], BF16) for i in range(2)])
            skn = Pool([sb("m2_skn%d" % i, [128, 16, TT], BF16) for i in range(2)])
            sv = Pool([sb("m2_sv%d" % i, [128, 2048], BF16) for i in range(2)])
            psq = Pool(self.ps[0:6])
            for c8 in range(8):
                S.dma("sp", Wkv.t[:, :, c8 * 512:(c8 + 1) * 512],
                      wkv.t[:, c8 * 512:(c8 + 1) * 512].rearrange("(kc p) c -> p kc c", p=128), r=[wkv.b], w=[Wkv.b])
            for tt in range(NT // TT):
                t0 = tt * TT
                c = cq.next()
                S.dma("sp", c.t[:, :, :], self.CQT[:, t0:t0 + TT].rearrange("(kc p) t -> p kc t", p=128), w=[c.b])
                rt = rp.next()
                S.dma("sp", rt.t[:, :, :], self.rope_mla_fm[:, :, t0:t0 + TT].rearrange("a p t -> p a t"), w=[rt.b])
                for h in range(16):
                    w = wsl.next()
                    S.dma("sp", w.t[:, :, :], wq.t[:, h * 256:(h + 1) * 256].rearrange("(kc p) c -> p kc c", p=128),
                          r=[wq.b], w=[w.b])
                    pa, pb, pc = psq.next(), psq.next(), psq.next()
                    for (pp, c0, m) in ((pa, 0, 128), (pb, 128, 64), (pc, 192, 64)):
                        for kc in range(12):
                            S.op("pe", lambda: nc.tensor.matmul(pp.t[0:m, :], lhsT=w.t[:, kc, c0:c0 + m], rhs=c.t[:, kc, :],
                                                                start=(kc == 0), stop=(kc == 11)),
                                 r=[w.b, c.b], w=[pp.b], inc=(kc == 11))
                    o1 = sn.next()
                    S.op("act", lambda: nc.scalar.copy(out=o1.t[:, :], in_=pa.t[:, :]), r=[pa.b], w=[o1.b])
                    t1, t2 = tq.next(), tq.next()
                    S.op("dve", lambda: nc.vector.tensor_tensor(out=t1.t[0:64, :], in0=pb.t[0:64, :], in1=rt.t[0:64, 0, :],
                                                                op=ALU.mult), r=[pb.b, rt.b], w=[t1.b])
                    S.op("dve", lambda: nc.vector.tensor_tensor(out=t2.t[0:64, :], in0=pc.t[0:64, :], in1=rt.t[0:64, 1, :],
                                                                op=ALU.mult), r=[pc.b, rt.b], w=[t2.b])
                    o2 = sr.next()
                    S.op("pool", lambda: nc.gpsimd.tensor_tensor(out=o2.t[0:64, :], in0=t1.t[0:64, :], in1=t2.t[0:64, :],
                                                                 op=ALU.add), r=[t1.b, t2.b], w=[o2.b])
                    S.dma("sp", self.QT2[h * 192:h * 192 + 128, t0:t0 + TT], o1.t[:, :], r=[o1.b])
                    S.dma("sp", self.QT2[h * 192 + 128:h * 192 + 192, t0:t0 + TT], o2.t[0:64, :], r=[o2.b])
            for seg in range(2):
                nkt = 4 if seg == 0 else 4 * NR
                KN = self.KNT_S if seg == 0 else self.KNT_P
                VT = self.VTM_S if seg == 0 else self.VTM_P
                for kt in range(nkt):
                    c = ck.next()
                    if seg == 0:
                        src = self.CKR_S[0:512, kt * TT:(kt + 1) * TT]
                        rd = []
                    else:
                        r_ = kt // 4
                        src = self.CKR_PG[kt % 4][r_ * 576:r_ * 576 + 512, :]
                        rd = [self.ckr_pg[kt % 4]]
                    S.dma("sp", c.t[:, :, :], src.rearrange("(kc p) t -> p kc t", p=128), r=rd, w=[c.b])
                    st = skn.next()
                    for h in range(16):
                        pp = psq.next()
                        for kc in range(4):
                            S.op("pe", lambda: nc.tensor.matmul(pp.t[:, :], lhsT=Wkv.t[:, kc, h * 128:(h + 1) * 128],
                                                                rhs=c.t[:, kc, :], start=(kc == 0), stop=(kc == 3)),
                                 r=[Wkv.b, c.b], w=[pp.b], inc=(kc == 3))
                        if h % 2 == 0:
                            S.op("act", lambda: nc.scalar.copy(out=st.t[:, h, :], in_=pp.t[:, :]), r=[pp.b], w=[st.b])
                        else:
                            S.op("dve", lambda: nc.vector.tensor_copy(out=st.t[:, h, :], in_=pp.t[:, :]), r=[pp.b],
                                 w=[st.b])
                    S.dma("sp", KN[:, kt * TT:(kt + 1) * TT].rearrange("(h p) t -> p h t", p=128), st.t[:, :, :], r=[st.b])
                    for kg in range(4):
                        so = sv.next()
                        for cs in range(4):
                            pp = psq.next()
                            for kc in range(4):
                                S.op("pe", lambda: nc.tensor.matmul(pp.t[:, :], lhsT=c.t[:, kc, kg * 128:(kg + 1) * 128],
                                                                    rhs=Wkv.t[:, kc, 2048 + cs * 512:2048 + (cs + 1) * 512],
                                                                    start=(kc == 0), stop=(kc == 3)),
                                     r=[Wkv.b, c.b], w=[pp.b], inc=(kc == 3))
                            if cs % 2 == 0:
                                S.op("act", lambda: nc.scalar.copy(out=so.t[:, cs * 512:(cs + 1) * 512], in_=pp.t[:, :]),
                                     r=[pp.b], w=[so.b])
                            else:
                                S.op("dve", lambda: nc.vector.tensor_copy(out=so.t[:, cs * 512:(cs + 1) * 512],
                                                                          in_=pp.t[:, :]), r=[pp.b], w=[so.b])
                        r0 = kt * TT + kg * 128
                        S.dma("sp", VT[r0:r0 + 128, :], so.t[:, :], r=[so.b])
            S.barrier()

    def phase_mla_attn(self):
        nc, S = self.nc, self.S
        with ExitStack() as es:
            sb = lambda n, s, d: self.sb(es, n, s, d)
            KnT = Pool([sb("m3_kn%d" % i, [128, NR * SEG], BF16) for i in range(2)])
            KrT = sb("m3_kr", [65, NR * SEG], BF16)
            Vh = Pool([sb("m3_v%d" % i, [128, NR * 16, 128], BF16) for i in range(2)])
            Qn = Pool([sb("m3_qn%d" % i, [128, SEG], BF16) for i in range(2)])
            Qr = Pool([sb("m3_qr%d" % i, [65, SEG], BF16) for i in range(2)])
            for q_ in Qr.items:
                S.op("dve", lambda: nc.vector.memset(q_.t[64:65, :], 1.0), w=[q_.b])
            PT = Pool([sb("m3_pt%d" % i, [128, TT], BF16) for i in range(3)])
            rden = Pool([sb("m3_rd%d" % i, [128, TT], F32) for i in range(2)])
            ost = Pool([sb("m3_os%d" % i, [128, TT], BF16) for i in range(2)])
            pS = Pool(self.ps[0:3])
            pO = Pool(self.ps[3:5])
            pD = Pool(self.ps[5:7])
            for seg in range(2):
                nk = SEG if seg == 0 else NR * SEG
                kk = 64 if seg == 0 else 65
                nkc = nk // 128
                KN = self.KNT_S if seg == 0 else self.KNT_P
                VT = self.VTM_S if seg == 0 else self.VTM_P
                if seg == 0:
                    S.dma("sp", KrT.t[0:64, 0:SEG], self.CKR_S[512:576, :], w=[KrT.b])
                else:
                    for r_ in range(NR):
                        for j in range(4):
                            S.dma("sp", KrT.t[0:64, r_ * SEG + j * TT:r_ * SEG + (j + 1) * TT],
                                  self.CKR_PG[j][r_ * 576 + 512:r_ * 576 + 576, :], r=[self.ckr_pg[j]], w=[KrT.b])
                        S.dma("pool", KrT.t[64:65, r_ * SEG:(r_ + 1) * SEG], self.kmask[r_:r_ + 1, :], w=[KrT.b])
                for h in range(16):
                    kn = KnT.next()
                    S.dma("sp", kn.t[:, 0:nk], KN[h * 128:(h + 1) * 128, :], w=[kn.b])
                    v = Vh.next()
                    S.dma("sp", v.t[:, 0:nkc, :], VT[:, h * 128:(h + 1) * 128].rearrange("(c p) v -> p c v", p=128),
                          w=[v.b])
                    qn, qr = Qn.next(), Qr.next()
                    c0 = seg * SEG
                    S.dma("sp", qn.t[:, :], self.QT2[h * 192:h * 192 + 128, c0:c0 + SEG], w=[qn.b])
                    S.dma("sp", qr.t[0:64, :], self.QT2[h * 192 + 128:h * 192 + 192, c0:c0 + SEG], w=[qr.b])
                    for qt in range(4):
                        qs = slice(qt * TT, (qt + 1) * TT)
                        po, pd = pO.next(), pD.next()

                        def scores(kc):
                            pp = pS.next()
                            ks = slice(kc * 128, (kc + 1) * 128)
                            S.op("pe", lambda: nc.tensor.matmul(pp.t[:, :], lhsT=kn.t[:, ks], rhs=qn.t[:, qs], start=True,
                                                                stop=False), r=[kn.b, qn.b], w=[pp.b], inc=False)
                            S.op("pe", lambda: nc.tensor.matmul(pp.t[:, :], lhsT=KrT.t[0:kk, ks], rhs=qr.t[0:kk, qs],
                                                                start=False, stop=True), r=[KrT.b, qr.b], w=[pp.b])
                            return pp

                        nxt = scores(0)
                        for kc in range(nkc):
                            cur = nxt
                            if kc + 1 < nkc:
                                nxt = scores(kc + 1)
                            pt = PT.next()
                            S.op("act", lambda: nc.scalar.activation(out=pt.t[:, :], in_=cur.t[:, :], func=AF.Exp,
                                                                     scale=MLA_SCALE), r=[cur.b], w=[pt.b])
                            S.op("pe", lambda: nc.tensor.matmul(po.t[:, :], lhsT=v.t[:, kc, :], rhs=pt.t[:, :],
                                                                start=(kc == 0), stop=(kc == nkc - 1)),
                                 r=[v.b, pt.b], w=[po.b], inc=False)
                            S.op("pe", lambda: nc.tensor.matmul(pd.t[:, :], lhsT=self.ones.t[:, :], rhs=pt.t[:, :],
                                                                start=(kc == 0), stop=(kc == nkc - 1)),
                                 r=[self.ones.b, pt.b], w=[pd.b], inc=True)
                        rd = rden.next()
                        S.op("dve", lambda: nc.vector.reciprocal(out=rd.t[:, :], in_=pd.t[:, :]), r=[pd.b], w=[rd.b])
                        o = ost.next()
                        S.op("dve", lambda: nc.vector.tensor_tensor(out=o.t[:, :], in0=po.t[:, :], in1=rd.t[:, :],
                                                                    op=ALU.mult), r=[po.b, rd.b], w=[o.b])
                        S.dma("sp", self.OT[h * 128:(h + 1) * 128, c0 + qt * TT:c0 + (qt + 1) * TT], o.t[:, :], r=[o.b])
            S.barrier()

    def phase_ffn_up(self, l):
        nc, S = self.nc, self.S
        wg = self.w_g["ffn_w_up"][l]
        with ExitStack() as es:
            sb = lambda n, s, d: self.sb(es, n, s, d)
            exs = sb("f1_exs", [2 * NR, D], F32)
            halo = sb("f1_halo", [128, 16, 2], BF16)
            cv = sb("f1_cv", [128, NFC * 4], F32)
            xT = Pool([sb("f1_x%d" % i, [128, 16, TT + 2], BF16) for i in range(2)])
            wsl = Pool([sb("f1_w%d" % i, [128, 16, 256], BF16) for i in range(6)])
            t1p = Pool([sb("f1_t%d" % i, [128, TT], F32) for i in range(2)])
            sp_ = Pool([sb("f1_s%d" % i, [128, TT], F32) for i in range(2)])
            hst = Pool([sb("f1_h%d" % i, [128, 2, TT], BF16) for i in range(3)])
            pG, pU, pH = Pool(self.ps[0:2]), Pool(self.ps[2:4]), Pool(self.ps[4:6])
            exin = Buf()
            exb = Buf()
            S.dma("sp", self.EXH_IN[0:1, :], self.X[SEG:SEG + 1, :], w=[exin])
            S.dma("sp", self.EXH_IN[1:2, :], self.X[NT - 1:NT, :], w=[exin])
            S.collective("AllGather", XG, self.EXH_IN.ap(), self.EXH.ap(), r=[exin], w=[exb])
            S.dma("sp", exs.t[0:2 * NR, :], self.EXH[:, :], r=[exb], w=[exs.b])
            S.dma("sp", cv.t[:, :], self.conv[l, :, :], w=[cv.b])
            ph = self.ps[6]
            for kc in range(16):
                S.op("pe", lambda: nc.tensor.matmul(ph.t[:, kc * 2:kc * 2 + 2], lhsT=exs.t[0:2 * NR, kc * 128:(kc + 1) * 128],
                                                    rhs=self.cst.t[0:2 * NR, C_SEL:C_SEL + 2], start=True, stop=True),
                     r=[exs.b, self.cst.b], w=[ph.b], inc=(kc == 15))
            S.op("dve", lambda: nc.vector.tensor_copy(out=halo.t[:, :, :],
                                                      in_=ph.t[:, 0:32].rearrange("p (a b) -> p a b", a=16)),
                 r=[ph.b], w=[halo.b])
            for tt in range(NT // TT):
                t0 = tt * TT
                seg, j = tt // 4, tt % 4
                x = xT.next()
                if j == 0:
                    S.dma("sp", x.t[:, :, 1:TT + 2], self.XT[:, t0:t0 + TT + 1].rearrange("(kc p) t -> p kc t", p=128),
                          w=[x.b])
                    edge = 0
                elif j == 3:
                    S.dma("sp", x.t[:, :, 0:TT + 1], self.XT[:, t0 - 1:t0 + TT].rearrange("(kc p) t -> p kc t", p=128),
                          w=[x.b])
                    edge = TT + 1
                else:
                    S.dma("sp", x.t[:, :, :], self.XT[:, t0 - 1:t0 + TT + 1].rearrange("(kc p) t -> p kc t", p=128),
                          w=[x.b])
                    edge = None
                if edge is not None:
                    if seg == 0:
                        S.op("dve", lambda: nc.vector.memset(x.t[:, :, edge:edge + 1], 0.0), w=[x.b])
                    else:
                        hc = 0 if j == 0 else 1
                        S.op("dve", lambda: nc.vector.tensor_copy(out=x.t[:, :, edge:edge + 1], in_=halo.t[:, :, hc:hc + 1]),
                             r=[halo.b], w=[x.b])
                xh = x.t.ap()[:, :, 0:TT + 2:TT + 1]
                for fc2 in range(NFC // 2):
                    wu, wgt = wsl.next(), wsl.next()
                    S.dma("sp", wu.t[:, :, :], wg.t[:, fc2 * 256:(fc2 + 1) * 256].rearrange("(kc p) c -> p kc c", p=128),
                          r=[wg.b], w=[wu.b])
                    S.dma("sp", wgt.t[:, :, :],
                          wg.t[:, FFN + fc2 * 256:FFN + (fc2 + 1) * 256].rearrange("(kc p) c -> p kc c", p=128),
                          r=[wg.b], w=[wgt.b])
                    hs = hst.next()
                    for jj in range(2):
                        fc = fc2 * 2 + jj
                        pg, pu, phh = pG.next(), pU.next(), pH.next()
                        for kc in range(16):
                            S.op("pe", lambda: nc.tensor.matmul(pg.t[:, :], lhsT=wgt.t[:, kc, jj * 128:(jj + 1) * 128],
                                                                rhs=x.t[:, kc, 1:TT + 1], start=(kc == 0), stop=(kc == 15)),
                                 r=[wgt.b, x.b], w=[pg.b], inc=(kc == 15))
                        for kc in range(16):
                            S.op("pe", lambda: nc.tensor.matmul(phh.t[:, 0:2], lhsT=wgt.t[:, kc, jj * 128:(jj + 1) * 128],
                                                                rhs=xh[:, kc, :], start=(kc == 0), stop=(kc == 15)),
                                 r=[wgt.b, x.b], w=[phh.b], inc=(kc == 15))
                        for kc in range(16):
                            S.op("pe", lambda: nc.tensor.matmul(pu.t[:, :], lhsT=wu.t[:, kc, jj * 128:(jj + 1) * 128],
                                                                rhs=x.t[:, kc, 1:TT + 1], start=(kc == 0), stop=(kc == 15)),
                                 r=[wu.b, x.b], w=[pu.b], inc=(kc == 15))
                        w0 = cv.t[:, fc * 4 + 0:fc * 4 + 1]
                        w1 = cv.t[:, fc * 4 + 1:fc * 4 + 2]
                        w2 = cv.t[:, fc * 4 + 2:fc * 4 + 3]
                        bb = cv.t[:, fc * 4 + 3:fc * 4 + 4]
                        t1 = t1p.next()
                        S.op("act", lambda: nc.scalar.activation(out=t1.t[:, :], in_=pg.t[:, :], func=AF.Identity, scale=w1,
                                                                 bias=bb), r=[pg.b, cv.b], w=[t1.b])
                        S.op("dve", lambda: nc.vector.scalar_tensor_tensor(out=t1.t[:, 1:TT], in0=pg.t[:, 0:TT - 1],
                                                                           scalar=w0, in1=t1.t[:, 1:TT], op0=ALU.mult,
                                                                           op1=ALU.add), r=[pg.b, cv.b, t1.b], w=[t1.b])
                        S.op("dve", lambda: nc.vector.scalar_tensor_tensor(out=t1.t[:, 0:TT - 1], in0=pg.t[:, 1:TT],
                                                                           scalar=w2, in1=t1.t[:, 0:TT - 1], op0=ALU.mult,
                                                                           op1=ALU.add), r=[pg.b, cv.b, t1.b], w=[t1.b])
                        S.op("dve", lambda: nc.vector.scalar_tensor_tensor(out=t1.t[:, 0:1], in0=phh.t[:, 0:1], scalar=w0,
                                                                           in1=t1.t[:, 0:1], op0=ALU.mult, op1=ALU.add),
                             r=[phh.b, cv.b, t1.b], w=[t1.b])
                        S.op("dve", lambda: nc.vector.scalar_tensor_tensor(out=t1.t[:, TT - 1:TT], in0=phh.t[:, 1:2],
                                                                           scalar=w2, in1=t1.t[:, TT - 1:TT], op0=ALU.mult,
                                                                           op1=ALU.add), r=[phh.b, cv.b, t1.b], w=[t1.b])
                        s_ = sp_.next()
                        S.op("act", lambda: nc.scalar.activation(out=s_.t[:, :], in_=t1.t[:, :], func=AF.Silu),
                             r=[t1.b], w=[s_.b])
                        S.op("dve", lambda: nc.vector.tensor_tensor(out=hs.t[:, jj, :], in0=s_.t[:, :], in1=pu.t[:, :],
                                                                    op=ALU.mult), r=[s_.b, pu.b], w=[hs.b])
                    S.dma("sp", self.HHT[fc2 * 256:(fc2 + 1) * 256, t0:t0 + TT].rearrange("(c p) t -> p c t", p=128),
                          hs.t[:, :, :], r=[hs.b])
            S.barrier()

    def build(self):
        nc, S = self.nc, self.S
        stop = self.stop_after
        es = ExitStack()
        self.setup_globals(es)
        self.prep_weights(self.order)

        def done(tag):
            return stop is not None and tag == stop

        def run():
            self.phase_xt0()
            if done("xt0"):
                return
            x_src = self.x_in
            for i in range(DEPTH):
                j = i // 2
                if i % 2 == 0:
                    self.phase_ret_proj(j)
                    if done("L%d_proj" % i):
                        return
                    self.phase_ret_core(j)
                    if done("L%d_core" % i):
                        return
                    self.phase_outproj_ln(self.OGT, 4096, self.w_g["ret_w_out"][j], x_src, self.X, i, 0)
                else:
                    self.phase_mla_c(j)
                    if done("L%d_c" % i):
                        return
                    self.phase_mla_qkv(j)
                    if done("L%d_qkv" % i):
                        return
                    self.phase_mla_attn()
                    if done("L%d_attn" % i):
                        return
                    self.phase_outproj_ln(self.OT, 2048, self.w_g["mla_w_out"][j], x_src, self.X, i, 0)
                x_src = self.X
                if done("L%d_ln1" % i):
                    return
                self.phase_ffn_up(i)
                if done("L%d_up" % i):
                    return
                last = (i == DEPTH - 1)
                self.phase_outproj_ln(self.HHT, FFN, self.w_g["ffn_w_down"][i], self.X, self.y_out if last else self.X,
                                      i, 2, write_xt=not last)
                if done("L%d_ln2" % i):
                    return

        run()
        for name in self.debug:
            src = getattr(self, name)
            shp = [int(v) for v in src.shape]
            o = nc.dram_tensor("dbg_" + name, shp, src.dtype, kind="ExternalOutput")
            rows = shp[0]
            step = max(1, (1 << 22) // (shp[1] * (4 if src.dtype == F32 else 2)))
            for r0 in range(0, rows, step):
                r1 = min(rows, r0 + step)
                S.dma("sp", o[r0:r1, :], src[r0:r1, :])
        S.barrier()
        es.close()
        return nc

    def _needed(self, o, stop):
        name, l = o
        li = int(stop[1]) if stop.startswith("L") else -1
        if stop == "xt0":
            return False
        if name.startswith("ffn"):
            wl = l
            late = stop.endswith(("_up", "_ln2"))
            return wl < li or (wl == li and late and (name == "ffn_w_up" or stop.endswith("_ln2")))
        wl = 2 * l if name.startswith("ret") else 2 * l + 1
        if wl < li:
            return True
        if wl > li:
            return False
        if name.endswith("_out"):
            return stop.endswith(("_ln1", "_up", "_ln2"))
        if name in ("mla_w_uq", "mla_w_ukv"):
            return not stop.endswith("_c")
        return True


def _rope_tables(pos):
    pos = pos.astype(np.float32)
    inv_r = (1.0 / (np.float32(10000.0) ** (np.arange(0, 128, dtype=np.float32) * np.float32(2.0 / 256)))).astype(np.float32)
    ang_r = (pos[:, None] * inv_r[None, :]).astype(np.float32).astype(np.float64)
    rope_ret = np.stack([np.cos(ang_r).T, np.sin(ang_r).T]).astype(np.float32)
    inv_m = (1.0 / (np.float32(10000.0) ** (np.arange(0, 32, dtype=np.float32) * np.float32(2.0 / 64)))).astype(np.float32)
    ang_m = (pos[:, None] * inv_m[None, :]).astype(np.float32).astype(np.float64)
    cm, sm = np.cos(ang_m), np.sin(ang_m)
    tm = np.concatenate([cm, sm], axis=1).astype(np.float32)
    cs = np.concatenate([cm, cm], axis=1).T
    sn = np.concatenate([-sm, sm], axis=1).T
    fm = np.stack([cs, sn]).astype(np.float32)
    return np.ascontiguousarray(rope_ret), np.ascontiguousarray(tm), np.ascontiguousarray(fm)


def _consts(core):
    c, g = core % 4, core // 4
    C = np.zeros((128, NCONST), np.float32)
    j = np.arange(128)[:, None].astype(np.float32)
    i = np.arange(128)[None, :].astype(np.float32)
    C[:, C_RELF:C_RELF + 128] = np.maximum(i - j, 0)
    C[:, C_MF:C_MF + 128] = (i >= j)
    C[:, C_RELB:C_RELB + 128] = np.maximum(j - i, 0)
    C[:, C_MB:C_MB + 128] = (j > i)
    C[:, C_IXF:C_IXF + 128] = i + 1
    C[:, C_IXB:C_IXB + 128] = 128 - i
    C[:, C_COLF] = 127 - np.arange(128)
    C[:, C_COLB] = np.arange(128)
    for rr in range(NR):
        r, same = rr % 4, (rr // 4 == g)
        C[:, C_EF + rr] = 2048.0 * (c - 1 - r) if (same and r < c) else 0.0
        C[:, C_MFC + rr] = 1.0 if (same and r < c) else 0.0
        C[:, C_EB + rr] = 2048.0 * (r - c - 1) if (same and r > c) else 0.0
        C[:, C_MBC + rr] = 1.0 if (same and r > c) else 0.0
    C[:, C_ID:C_ID + 128] = np.eye(128, dtype=np.float32)
    if c > 0:
        C[2 * (core - 1) + 1, C_SEL + 0] = 1.0
    if c < 3:
        C[2 * (core + 1), C_SEL + 1] = 1.0
    return C


def make_in_maps(inp, n=8, nsh=8):
    f = lambda a: np.ascontiguousarray(np.asarray(a, dtype=np.float32))
    uq = f(inp["mla_w_uq"]).reshape(2, 1536, 16, 192)
    uq_ext = np.concatenate([uq[..., 0:192], uq[..., 160:192], uq[..., 128:160]], axis=-1).reshape(2, 1536, 4096)
    ukv = f(inp["mla_w_ukv"]).reshape(2, 512, 16, 256)
    ukv_p = np.concatenate([ukv[..., 0:128].reshape(2, 512, 2048), ukv[..., 128:256].reshape(2, 512, 2048)], axis=-1)
    W = {"ret_w_in": f(inp["ret_w_in"]), "ret_w_out": f(inp["ret_w_out"]), "mla_w_in": f(inp["mla_w_in"]),
         "mla_w_uq": uq_ext, "mla_w_ukv": ukv_p, "mla_w_out": f(inp["mla_w_out"]),
         "ffn_w_up": f(inp["ffn_w_up"]), "ffn_w_down": f(inp["ffn_w_down"])}
    ret_decay = np.concatenate([f(inp["ret_decay_fwd"]), f(inp["ret_decay_bwd"])], axis=1)
    ln = np.stack([f(inp["ln1_g"]), f(inp["ln1_b"]), f(inp["ln2_g"]), f(inp["ln2_b"])], axis=1)
    cw = f(inp["ffn_conv_w"]).reshape(4, 3, NFC, 128)
    cb = f(inp["ffn_conv_b"]).reshape(4, 1, NFC, 128)
    conv = np.ascontiguousarray(np.concatenate([cw, cb], axis=1).transpose(0, 3, 2, 1)).reshape(4, 128, NFC * 4)
    xs, xp = f(inp["x_sample"]), f(inp["x_prompt"])
    rep = lambda a: np.ascontiguousarray(np.broadcast_to(a[..., None, :], a.shape[:-1] + (128, a.shape[-1])))
    maps = []
    for c in range(n):
        q = c % 4
        m = {"x": np.ascontiguousarray(np.concatenate([xs[c], xp[c // 4, q * SEG:(q + 1) * SEG]], axis=0))}
        for name, w in W.items():
            K = w.shape[1]
            for l in range(w.shape[0]):
                m["%s_%d" % (name, l)] = np.ascontiguousarray(w[l, (c % nsh) * (K // nsh):(c % nsh + 1) * (K // nsh), :])
        m["ret_decay"] = rep(ret_decay)
        m["mla_q_norm"] = rep(f(inp["mla_q_norm"]))
        m["mla_kv_norm"] = rep(f(inp["mla_kv_norm"]))
        m["ln"] = rep(ln)
        m["conv"] = conv
        pos = np.concatenate([np.arange(SEG), q * SEG + np.arange(SEG)])
        rr, tm, fm = _rope_tables(pos)
        m["rope_ret"], m["rope_mla_tm"], m["rope_mla_fm"] = rr, tm, fm
        m["consts"] = _consts(c)
        km = np.full((NR, SEG), KMASK, np.float32)
        km[(c // 4) * 4:(c // 4) * 4 + 4, :] = 0.0
        m["kmask"] = km
        maps.append(m)
    return maps


_NC_CACHE = {}


def kernel(**inputs):
    if "nc" not in _NC_CACHE:
        _NC_CACHE["nc"] = Builder().build()
    nc = _NC_CACHE["nc"]
    maps = make_in_maps(inputs)
    res = run_bass_kernel_spmd(nc, maps, core_ids=list(range(8)))
    y_s = np.empty((8, SEG, D), np.float32)
    y_p = np.empty((2, 4 * SEG, D), np.float32)
    for c in range(8):
        y = np.asarray(res.results[c]["y"], dtype=np.float32)
        y_s[c] = y[0:SEG]
        y_p[c // 4, (c % 4) * SEG:(c % 4 + 1) * SEG] = y[SEG:NT]
    return (y_p, y_s)
```
